# Optimizing a Trainium2 kernel written in Bass

```python
import jax, jax.numpy as jnp
from jax import lax
import numpy as np

D_MODEL = 4096
BATCH = 4
SEQ = 2048
DEPTH = 2

CHUNK = 64
N_META = 16
N_MIXERS = 2
N_RWKV_LAYERS = (DEPTH + 1) // 2
N_SB_LAYERS = DEPTH // 2
RWKV_HEAD = 64
RWKV_HEADS = D_MODEL // RWKV_HEAD
N_MU = 6
D_DECAY_LORA = 128
D_AAA_LORA = 128
D_GATE_LORA = 480
SB_HEAD = 128
SB_HEADS = D_MODEL // SB_HEAD
Q_BLOCK = 128
D_FF = 4 * D_MODEL
LN_EPS = 1e-5
GN_EPS = 64e-5
L2_EPS = 1e-12
DEEPNORM_ALPHA = (2 * DEPTH) ** 0.25
DEEPNORM_BETA = (8 * DEPTH) ** -0.25

kernel_name = "hybrid_rwkv7_stickbreaking_deepnorm"


def layer_norm(x, g, b):
    xf = x.astype(jnp.float32)
    mu = jnp.mean(xf, axis=-1, keepdims=True)
    var = jnp.mean(jnp.square(xf - mu), axis=-1, keepdims=True)
    return ((xf - mu) * lax.rsqrt(var + LN_EPS)).astype(x.dtype) * g + b


def token_shift(x):
    return jnp.pad(x[:, :-1], ((0, 0), (1, 0), (0, 0)))


def rwkv7_time_mix(x, mu, w_rkv, w0, w1, w2, a0, a1, a2, g1, g2,
                   k_k, k_a, r_k, gn_g, gn_b, w_o):
    B, T, D = x.shape
    H, N = RWKV_HEADS, RWKV_HEAD
    f32 = jnp.float32
    xx = token_shift(x) - x
    mixed = x[None] + xx[None] * mu[:, None, None, :]
    rkv = jnp.einsum('nbtd,nde->nbte', mixed[:3], w_rkv)
    r, k, v = rkv[0], rkv[1], rkv[2]
    xw, xa, xg = mixed[3], mixed[4], mixed[5]

    w_log = -jax.nn.softplus(-(w0 + jnp.tanh(xw @ w1) @ w2)) - 0.5
    decay = jnp.exp(-jnp.exp(w_log.astype(f32)))
    a = jax.nn.sigmoid(a0 + (xa @ a1) @ a2)
    g = jax.nn.sigmoid(xg @ g1) @ g2

    def heads(z):
        return z.reshape(B, T, H, N).astype(f32)

    kk = heads(k * k_k)
    kk = kk / jnp.maximum(jnp.sqrt(jnp.sum(jnp.square(kk), axis=-1, keepdims=True)), L2_EPS)
    k = k * (1.0 + (a - 1.0) * k_a)
    r_h, k_h, v_h, a_h, w_h = heads(r), heads(k), heads(v), heads(a), decay.reshape(B, T, H, N)

    def step(S, inp):
        r_t, w_t, k_t, v_t, kk_t, a_t = inp
        s_kk = jnp.einsum('bhij,bhj->bhi', S, kk_t)
        S = (S * w_t[:, :, None, :]
             - jnp.einsum('bhi,bhj->bhij', s_kk, kk_t * a_t)
             + jnp.einsum('bhi,bhj->bhij', v_t, k_t))
        return S, jnp.einsum('bhij,bhj->bhi', S, r_t)

    xs = tuple(jnp.moveaxis(z, 1, 0) for z in (r_h, w_h, k_h, v_h, kk, a_h))
    S0 = jnp.zeros((B, H, N, N), f32)
    _, ys = lax.scan(step, S0, xs)
    y = jnp.moveaxis(ys, 0, 1)

    mu_y = jnp.mean(y, axis=-1, keepdims=True)
    var_y = jnp.mean(jnp.square(y - mu_y), axis=-1, keepdims=True)
    y_n = ((y - mu_y) * lax.rsqrt(var_y + GN_EPS)).reshape(B, T, D) * gn_g + gn_b
    bonus = (jnp.sum(r_h * k_h * r_k, axis=-1, keepdims=True) * v_h).reshape(B, T, D)
    return ((y_n + bonus).astype(x.dtype) * g) @ w_o


def stick_breaking_attention(x, w_qkv, w_o):
    B, T, D = x.shape
    H, Dh = SB_HEADS, SB_HEAD
    qkv = (x @ w_qkv).reshape(B, T, 3, H, Dh)
    q, k, v = qkv[:, :, 0], qkv[:, :, 1], qkv[:, :, 2]
    scale = Dh ** -0.5
    n_blocks = -(-T // Q_BLOCK)
    outs = []
    for blk in range(n_blocks):
        t0 = blk * Q_BLOCK
        t1 = min(T, t0 + Q_BLOCK)
        kb, vb = k[:, :t1], v[:, :t1]
        z = jnp.einsum('bqhd,bkhd->bhqk', q[:, t0:t1], kb).astype(jnp.float32) * scale
        causal = jnp.arange(t1)[None, :] < jnp.arange(t0, t1)[:, None]
        log_keep = jnp.where(causal, jax.nn.log_sigmoid(-z), 0.0)
        suffix = lax.cumsum(log_keep, axis=3, reverse=True) - log_keep
        attn = jnp.where(causal, jnp.exp(jax.nn.log_sigmoid(z) + suffix), 0.0)
        outs.append(jnp.einsum('bhqk,bkhd->bqhd', attn.astype(v.dtype), vb))
    o = jnp.concatenate(outs, axis=1).reshape(B, T, D)
    return o @ w_o


def squared_relu_mlp(x, w_up, w_down):
    return jnp.square(jax.nn.relu(x @ w_up)) @ w_down


def setup_inputs(seed: int = 0) -> dict:
    key = jax.random.key(seed)
    ks = jax.random.split(key, 32)
    D, R, S = D_MODEL, N_RWKV_LAYERS, N_SB_LAYERS
    nrm = jax.random.normal
    uni = jax.random.uniform
    f32 = jnp.float32
    return {
        'x': nrm(ks[0], (BATCH, SEQ, D), f32),
        'meta_tokens': nrm(ks[1], (N_META, D), f32),
        'ln_mix_g': 1.0 + 0.02 * nrm(ks[2], (DEPTH, D), f32),
        'ln_mix_b': 0.02 * nrm(ks[3], (DEPTH, D), f32),
        'ln_ffn_g': 1.0 + 0.02 * nrm(ks[4], (DEPTH, D), f32),
        'ln_ffn_b': 0.02 * nrm(ks[5], (DEPTH, D), f32),
        'w_up': nrm(ks[6], (DEPTH, D, D_FF), f32) * D ** -0.5,
        'w_down': nrm(ks[7], (DEPTH, D_FF, D), f32) * (DEEPNORM_BETA * D_FF ** -0.5),
        'rwkv_mu': uni(ks[8], (R, N_MU, D), f32),
        'rwkv_w_rkv': nrm(ks[9], (R, 3, D, D), f32) * D ** -0.5,
        'rwkv_w0': -6.0 + 5.0 * uni(ks[10], (R, D), f32),
        'rwkv_w1': nrm(ks[11], (R, D, D_DECAY_LORA), f32) * D ** -0.5,
        'rwkv_w2': nrm(ks[12], (R, D_DECAY_LORA, D), f32) * (0.5 * D_DECAY_LORA ** -0.5),
        'rwkv_a0': 0.1 * nrm(ks[13], (R, D), f32),
        'rwkv_a1': nrm(ks[14], (R, D, D_AAA_LORA), f32) * D ** -0.5,
        'rwkv_a2': nrm(ks[15], (R, D_AAA_LORA, D), f32) * (0.5 * D_AAA_LORA ** -0.5),
        'rwkv_g1': nrm(ks[16], (R, D, D_GATE_LORA), f32) * D ** -0.5,
        'rwkv_g2': nrm(ks[17], (R, D_GATE_LORA, D), f32) * D_GATE_LORA ** -0.5,
        'rwkv_k_k': 0.85 + 0.05 * nrm(ks[18], (R, D), f32),
        'rwkv_k_a': 1.0 + 0.05 * nrm(ks[19], (R, D), f32),
        'rwkv_r_k': 0.1 * nrm(ks[20], (R, RWKV_HEADS, RWKV_HEAD), f32),
        'rwkv_gn_g': 1.0 + 0.02 * nrm(ks[21], (R, D), f32),
        'rwkv_gn_b': 0.02 * nrm(ks[22], (R, D), f32),
        'rwkv_w_o': nrm(ks[23], (R, D, D), f32) * (DEEPNORM_BETA * D ** -0.5),
        'sb_w_qkv': nrm(ks[24], (S, D, 3 * D), f32) * D ** -0.5,
        'sb_w_o': nrm(ks[25], (S, D, D), f32) * (DEEPNORM_BETA * D ** -0.5),
    }


def reference(x, meta_tokens, ln_mix_g, ln_mix_b, ln_ffn_g, ln_ffn_b, w_up, w_down,
              rwkv_mu, rwkv_w_rkv, rwkv_w0, rwkv_w1, rwkv_w2, rwkv_a0, rwkv_a1, rwkv_a2,
              rwkv_g1, rwkv_g2, rwkv_k_k, rwkv_k_a, rwkv_r_k, rwkv_gn_g, rwkv_gn_b, rwkv_w_o,
              sb_w_qkv, sb_w_o):
    B = x.shape[0]
    meta = jnp.broadcast_to(meta_tokens[None].astype(x.dtype), (B, N_META, D_MODEL))
    h = jnp.concatenate([meta, x], axis=1)
    for i in range(DEPTH):
        j = i // N_MIXERS
        if i % N_MIXERS == 0:
            mix = rwkv7_time_mix(h, rwkv_mu[j], rwkv_w_rkv[j], rwkv_w0[j], rwkv_w1[j], rwkv_w2[j],
                                 rwkv_a0[j], rwkv_a1[j], rwkv_a2[j], rwkv_g1[j], rwkv_g2[j],
                                 rwkv_k_k[j], rwkv_k_a[j], rwkv_r_k[j], rwkv_gn_g[j], rwkv_gn_b[j],
                                 rwkv_w_o[j])
        else:
            mix = stick_breaking_attention(h, sb_w_qkv[j], sb_w_o[j])
        h = layer_norm(DEEPNORM_ALPHA * h + mix, ln_mix_g[i], ln_mix_b[i])
        h = layer_norm(DEEPNORM_ALPHA * h + squared_relu_mlp(h, w_up[i], w_down[i]),
                       ln_ffn_g[i], ln_ffn_b[i])
    return h[:, N_META:]
```

```python
import numpy as np
import concourse.bass as bass
import concourse.mybir as mybir
from concourse.bass_utils import run_bass_kernel_spmd

F32 = mybir.dt.float32
BF16 = mybir.dt.bfloat16
ALU = mybir.AluOpType
AF = mybir.ActivationFunctionType
AX = mybir.AxisListType

ENGS = ("pe", "act", "dve", "pool", "sp")


class Prog:
    def __init__(self, nc):
        self.nc = nc
        self.lists = {e: [] for e in ENGS}
        self.cnt = {}
        self.known = {e: {} for e in ENGS}
        self._semctx = []
        self.esem = {}
        for e in ("pe", "act", "dve", "pool"):
            self.esem[e] = self.new_sem("s_" + e)
            self.cnt[e] = 0
        self.dsem = {}
        self.lastw = {}
        self.reads = {}
        self.nops = {e: 0 for e in ENGS}

    def new_sem(self, name):
        cm = self.nc.semaphore(name)
        s = cm.__enter__()
        self._semctx.append(cm)
        return s

    def _wait(self, eng, tk):
        if tk is None:
            return
        sem, val, src = tk
        if src == eng and eng == "pe":
            return
        key = id(sem)
        if self.known[eng].get(key, 0) >= val:
            return
        self.known[eng][key] = val
        self.lists[eng].append(("w", sem, val))

    def _hazards(self, eng, reads, writes):
        for b in reads:
            self._wait(eng, self.lastw.get(b))
        for b in writes:
            self._wait(eng, self.lastw.get(b))
            for tk in self.reads.get(b, ()):
                self._wait(eng, tk)

    def _commit(self, tk, reads, writes):
        for b in reads:
            self.reads.setdefault(b, []).append(tk)
        for b in writes:
            self.lastw[b] = tk
            self.reads[b] = []

    def op(self, eng, fns, reads=(), writes=(), extra=()):
        if callable(fns):
            fns = [fns]
        self._hazards(eng, reads, writes)
        for tk in extra:
            self._wait(eng, tk)
        self.cnt[eng] += 1
        tk = (self.esem[eng], self.cnt[eng], eng)
        for f in fns[:-1]:
            self.lists[eng].append(("i", f, None, 0))
        self.lists[eng].append(("i", fns[-1], self.esem[eng], 1))
        self.nops[eng] += len(fns)
        self._commit(tk, reads, writes)
        return tk

    def dma(self, eng, key, outs_ins, reads=(), writes=(), extra=(), **kw):
        if key not in self.dsem:
            self.dsem[key] = [self.new_sem("d_" + key.replace(":", "_")), 0]
        st = self.dsem[key]
        self._hazards(eng, reads, writes)
        for tk in extra:
            self._wait(eng, tk)
        for (o, i) in outs_ins:
            st[1] += 16
            self.lists[eng].append(("i", (lambda e, o=o, i=i: e.dma_start(out=o, in_=i, **kw)), st[0], 16))
            self.nops[eng] += 1
        tk = (st[0], st[1], "dma")
        self._commit(tk, reads, writes)
        return tk

    def build(self, final_eng="sp"):
        nc = self.nc
        for key, st in self.dsem.items():
            if st[1] > 0:
                self._wait(final_eng, (st[0], st[1], "dma"))
        lists = self.lists

        def replay(lst):
            def f(e):
                for h in lst:
                    if h[0] == "w":
                        e.wait_ge(h[1], h[2])
                    else:
                        ins = h[1](e)
                        if h[2] is not None:
                            ins.then_inc(h[2], h[3])
            return f

        with nc.Block() as block:
            block.tensor(replay(lists["pe"]))
            block.scalar(replay(lists["act"]))
            block.vector(replay(lists["dve"]))
            block.gpsimd(replay(lists["pool"]))
            block.sync(replay(lists["sp"]))
        for cm in reversed(self._semctx):
            cm.__exit__(None, None, None)


ALPHA = 4.0 ** 0.25
LN_EPS = 1e-5


class Arena:
    def __init__(self, nc, words):
        self.t = nc.alloc_sbuf_tensor("arena", [128, words], F32)
        self.words = words
        self.off = 0

    def reset(self, off=0):
        self.off = off

    def take(self, shape, dtype):
        n = 1
        for s in shape[1:]:
            n *= s
        nw = n if dtype == F32 else (n + 1) // 2
        assert self.off + nw <= self.words, (self.off, nw, self.words)
        v = self.t[:, self.off:self.off + nw]
        self.off += nw
        if dtype != F32:
            v = v.bitcast(dtype)
        if len(shape) == 3:
            v = v.rearrange("p (a b) -> p a b", a=shape[1])
        return v


def fence(P, only=None):
    tks = [(P.esem[e], P.cnt[e], e) for e in ("pe", "act", "dve", "pool") if P.cnt[e] > 0]
    import os
    if os.environ.get("FENCE_NODMA") != "1":
        tks += [(st[0], st[1], "dma") for st in P.dsem.values() if st[1] > 0]
    for e in (ENGS if not only else only.split(",")):
        for tk in tks:
            if tk[2] == e:
                continue
            P._wait(e, tk)


def tok_chunks(tiles, maxn=512):
    out = []
    for (t0, nt) in tiles:
        if out and out[-1][1] + nt <= maxn and out[-1][0] + out[-1][1] == t0 and nt == 128 and out[-1][1] % 128 == 0:
            out[-1] = (out[-1][0], out[-1][1] + nt)
        else:
            out.append((t0, nt))
    return out


def load_slab(P, slab_views, si, w_ap, KT, ncols):
    s = si % len(slab_views)
    wv = w_ap.rearrange("(kt p) n -> p kt n", p=128)
    step = max(1, KT // 4)
    pairs = [(slab_views[s][:, q:q + step, :ncols], wv[:, q:q + step, :]) for q in range(0, KT, step)]
    P.dma("pool", f"slab{s}", pairs, writes=[f"slab{s}"])
    return s


def layer_norm_tile(P, nc, u, nt, D, stats, mv, rstd, gb, bb, eps):
    uname, u_ap = u
    nch = D // 512
    fns = [(lambda e, c=c: e.bn_stats(out=stats[:nt, c * 6:(c + 1) * 6], in_=u_ap[:nt, c * 512:(c + 1) * 512])) for c in range(nch)]
    P.op("dve", fns, reads=[uname], writes=["ln_stats"])
    P.op("dve", lambda e: e.bn_aggr(out=mv[:nt, :], in_=stats[:nt, :nch * 6]), reads=["ln_stats"], writes=["ln_mv"])
    P.op("act", lambda e: e.activation(out=rstd[:nt, 0:1], in_=mv[:nt, 1:2], func=AF.Sqrt, bias=eps_ap(nc)[:nt, :], scale=1.0),
         reads=["ln_mv"], writes=["ln_rstd"])
    P.op("dve", lambda e: e.reciprocal(out=rstd[:nt, 0:1], in_=rstd[:nt, 0:1]), reads=["ln_rstd"], writes=["ln_rstd"])
    P.op("dve", lambda e: e.tensor_scalar(out=u_ap[:nt, :], in0=u_ap[:nt, :], scalar1=mv[:nt, 0:1], scalar2=rstd[:nt, 0:1],
                                          op0=ALU.subtract, op1=ALU.mult), reads=[uname, "ln_mv", "ln_rstd"], writes=[uname])
    P.op("pool", lambda e: e.tensor_tensor(out=u_ap[:nt, :], in0=u_ap[:nt, :], in1=gb[:nt, :], op=ALU.mult),
         reads=[uname, "ln_g"], writes=[uname])
    P.op("dve", lambda e: e.tensor_tensor(out=u_ap[:nt, :], in0=u_ap[:nt, :], in1=bb[:nt, :], op=ALU.add),
         reads=[uname, "ln_b"], writes=[uname])


_consts = {}


def eps_ap(nc):
    return _consts["eps"]


def setup_consts(P, nc, eps=LN_EPS):
    ident = nc.alloc_sbuf_tensor("ident", [128, 128], F32)
    epst = nc.alloc_sbuf_tensor("epst", [128, 1], F32)
    _consts["ident"] = ident
    _consts["eps"] = epst
    P.op("pool", lambda e: e.memset(ident[:], 0.0), writes=["ident"])
    P.op("pool", lambda e: e.affine_select(out=ident[:], in_=ident[:], pattern=[[-1, 128]], compare_op=ALU.not_equal,
                                           fill=1.0, base=0, channel_multiplier=1), reads=["ident"], writes=["ident"])
    P.op("pool", lambda e: e.memset(epst[:], eps), writes=["eps"])


def transpose_to_xT(P, nc, src, nt, KT, xT, col0, psnames, pstiles, cnt, xname="xT"):
    sname, s_ap = src
    ident = _consts["ident"]
    for k0 in range(0, KT, 4):
        pi = cnt[0] % len(pstiles)
        cnt[0] += 1
        ps = pstiles[pi]
        fns = [(lambda e, j=j, ps=ps, k0=k0: e.transpose(out=ps[:, j * 128:j * 128 + nt], in_=s_ap[:nt, (k0 + j) * 128:(k0 + j + 1) * 128],
                                                         identity=ident[:nt, :nt])) for j in range(4)]
        P.op("pe", fns, reads=[sname, "ident"], writes=[psnames[pi]])
        eng = "act" if (cnt[0] % 2) else "dve"
        if eng == "act":
            f = lambda e, ps=ps, k0=k0: e.copy(out=xT[:, k0:k0 + 4, col0:col0 + nt], in_=ps[:, :].rearrange("p (a b) -> p a b", a=4)[:, :, :nt])
        else:
            f = lambda e, ps=ps, k0=k0: e.tensor_copy(out=xT[:, k0:k0 + 4, col0:col0 + nt], in_=ps[:, :].rearrange("p (a b) -> p a b", a=4)[:, :, :nt])
        P.op(eng, f, reads=[psnames[pi]], writes=[xname])


def emit_dense(P, nc, A, PS, pfx, D, F, NE, tiles, zT, h, w_o, g1, b1, w_up, w_down, g2, b2, out,
               u1, h1s, parts, alpha=ALPHA):
    KT = D // 128
    ntok = sum(nt for _, nt in tiles)
    FE = F // NE
    FT = FE // 128
    psn = [f"ps{i}" for i in range(8)]
    chunks = tok_chunks(tiles)
    ND = D // 512

    fence(P)
    A.reset()
    xT = A.take([128, KT, ntok], BF16)
    base_after_xT = A.off
    slabs = [A.take([128, KT, 512], BF16) for _ in range(2)]
    stage = [A.take([128, 512], F32) for _ in range(2)]
    hres = [A.take([128, 512], F32) for _ in range(2)]
    zv = zT.rearrange("(kt p) n -> p kt n", p=128)
    step = max(1, KT // 4)
    P.dma("pool", "xT", [(xT[:, q:q + step, :], zv[:, q:q + step, :]) for q in range(0, KT, step)], writes=["xT"])
    it = 0
    for c in range(ND):
        s = load_slab(P, slabs, c, w_o[:, c * 512:(c + 1) * 512], KT, 512)
        for (t0, nt) in tiles:
            b = it % 2
            pi = it % 4
            it += 1
            P.dma("sp", f"hres{b}", [(hres[b][:nt, :], h[t0:t0 + nt, c * 512:(c + 1) * 512])], writes=[f"hres{b}"])
            fns = [(lambda e, kt=kt, t0=t0, nt=nt, s=s, pi=pi: e.matmul(PS[pi][:nt, :], lhsT=xT[:, kt, t0:t0 + nt], rhs=slabs[s][:, kt, :],
                                                                      start=(kt == 0), stop=(kt == KT - 1))) for kt in range(KT)]
            P.op("pe", fns, reads=["xT", f"slab{s}"], writes=[psn[pi]])
            P.op("dve", lambda e, b=b, nt=nt, pi=pi: e.scalar_tensor_tensor(out=stage[b][:nt, :], in0=hres[b][:nt, :], scalar=alpha,
                                                                           in1=PS[pi][:nt, :], op0=ALU.mult, op1=ALU.add),
                 reads=[f"hres{b}", psn[pi]], writes=[f"stage{b}"])
            P.dma("sp", f"stage{b}:st", [(u1[t0:t0 + nt, c * 512:(c + 1) * 512], stage[b][:nt, :])], reads=[f"stage{b}"])

    import os
    if os.environ.get("STOP") == "1":
        return
    fence(P)
    A.reset(base_after_xT)
    ub2 = [A.take([128, D], F32) for _ in range(2)]
    gb2 = A.take([128, D], F32)
    bb2 = A.take([128, D], F32)
    stats2 = A.take([128, 64], F32)
    mv2 = A.take([128, 2], F32)
    rstd2 = A.take([128, 2], F32)
    P.dma("sp", "ln_g", [(gb2[:, :], g1.partition_broadcast(128))], writes=["ln_g"])
    P.dma("sp", "ln_b", [(bb2[:, :], b1.partition_broadcast(128))], writes=["ln_b"])
    tcnt = [0]
    for i, (t0, nt) in enumerate(tiles):
        b = i % 2
        P.dma("sp", f"ub{b}", [(ub2[b][:nt, :], u1[t0:t0 + nt, :])], writes=[f"ub{b}"])
        LNS = int(os.environ.get("LNS", "9"))
        if LNS >= 2:
            layer_norm_tile(P, nc, (f"ub{b}", ub2[b]), nt, D, stats2, mv2, rstd2, gb2, bb2, LN_EPS)
        P.dma("sp", f"ub{b}:st", [(h1s[t0:t0 + nt, :], ub2[b][:nt, :])], reads=[f"ub{b}"])
        if LNS >= 3:
            transpose_to_xT(P, nc, (f"ub{b}", ub2[b]), nt, KT, xT, t0, psn[4:8], PS[4:8], tcnt)

    if os.environ.get("STOP") == "2":
        return
    fence(P, os.environ.get("LASTF"))
    A.reset(base_after_xT)
    hidT = A.take([128, FT, ntok], BF16)
    slabs3 = [A.take([128, KT, 512], BF16) for _ in range(2)]
    stage3 = [A.take([128, 512], F32) for _ in range(2)]
    rtmp = [A.take([128, 512], F32) for _ in range(2)]
    si = 0
    it = 0
    for e_ in range(NE):
        f0 = e_ * FE
        for sl in range(0 if os.environ.get("NOUP") == "1" else FE // 512):
            s = load_slab(P, slabs3, si, w_up[:, f0 + sl * 512:f0 + (sl + 1) * 512], KT, 512)
            si += 1
            for f4 in range(4):
                ft = sl * 4 + f4
                for (c0, cn) in chunks:
                    b = it % 2
                    pi = it % 4
                    it += 1
                    fns = [(lambda e, kt=kt, s=s, f4=f4, c0=c0, cn=cn, pi=pi: e.matmul(PS[pi][:, :cn], lhsT=slabs3[s][:, kt, f4 * 128:(f4 + 1) * 128],
                                                                                    rhs=xT[:, kt, c0:c0 + cn], start=(kt == 0), stop=(kt == KT - 1)))
                           for kt in range(KT)]
                    LV = int(os.environ.get("LV", "9"))
                    if LV < 1: continue
                    P.op("pe", fns, reads=["xT", f"slab{s}"], writes=[psn[pi]])
                    if LV < 2: continue
                    P.op("act", lambda e, b=b, cn=cn, pi=pi: e.activation(out=rtmp[b][:, :cn], in_=PS[pi][:, :cn], func=AF.Relu),
                         reads=[psn[pi]], writes=[f"rtmp{b}"])
                    if LV < 3: continue
                    P.op("dve", lambda e, b=b, cn=cn, c0=c0, ft=ft: e.tensor_tensor(out=hidT[:, ft, c0:c0 + cn], in0=rtmp[b][:, :cn], in1=rtmp[b][:, :cn], op=ALU.mult),
                         reads=[f"rtmp{b}"], writes=["hidT"])
        if os.environ.get("NODOWN") == "1":
            continue
        for c in range(ND):
            s = si % 2
            si += 1
            wv = w_down[f0:f0 + FE, c * 512:(c + 1) * 512].rearrange("(kt p) n -> p kt n", p=128)
            stp = max(1, FT // 4)
            P.dma("pool", f"slab{s}", [(slabs3[s][:, q:q + stp, :], wv[:, q:q + stp, :]) for q in range(0, FT, stp)], writes=[f"slab{s}"])
            for (t0, nt) in tiles:
                b = it % 2
                pi = it % 4
                it += 1
                fns = [(lambda e, ft=ft, t0=t0, nt=nt, s=s, pi=pi: e.matmul(PS[pi][:nt, :], lhsT=hidT[:, ft, t0:t0 + nt], rhs=slabs3[s][:, ft, :],
                                                                          start=(ft == 0), stop=(ft == FT - 1))) for ft in range(FT)]
                P.op("pe", fns, reads=["hidT", f"slab{s}"], writes=[psn[pi]])
                P.op("act", lambda e, b=b, nt=nt, pi=pi: e.copy(out=stage3[b][:nt, :], in_=PS[pi][:nt, :]), reads=[psn[pi]], writes=[f"stage{b}"])
                P.dma("sp", f"stage{b}:st", [(parts[e_][t0:t0 + nt, c * 512:(c + 1) * 512], stage3[b][:nt, :])], reads=[f"stage{b}"])

    if os.environ.get("STOP") == "3":
        return
    fence(P)
    A.reset()
    ub4 = [A.take([128, D], F32) for _ in range(2)]
    pb4 = [A.take([128, D], F32) for _ in range(3)]
    gb4 = A.take([128, D], F32)
    bb4 = A.take([128, D], F32)
    stats4 = A.take([128, 64], F32)
    mv4 = A.take([128, 2], F32)
    rstd4 = A.take([128, 2], F32)
    P.dma("sp", "ln_g", [(gb4[:, :], g2.partition_broadcast(128))], writes=["ln_g"])
    P.dma("sp", "ln_b", [(bb4[:, :], b2.partition_broadcast(128))], writes=["ln_b"])
    pit = 0
    for i, (t0, nt) in enumerate(tiles):
        b = i % 2
        P.dma("sp", f"ub{b}", [(ub4[b][:nt, :], h1s[t0:t0 + nt, :])], writes=[f"ub{b}"])
        for e_ in range(NE):
            q = pit % 3
            pit += 1
            P.dma("sp", f"pb{q}", [(pb4[q][:nt, :], parts[e_][t0:t0 + nt, :])], writes=[f"pb{q}"])
            if e_ == 0:
                P.op("dve", lambda e, b=b, q=q, nt=nt: e.scalar_tensor_tensor(out=ub4[b][:nt, :], in0=ub4[b][:nt, :], scalar=alpha, in1=pb4[q][:nt, :],
                                                                             op0=ALU.mult, op1=ALU.add), reads=[f"ub{b}", f"pb{q}"], writes=[f"ub{b}"])
            else:
                P.op("dve", lambda e, b=b, q=q, nt=nt: e.tensor_tensor(out=ub4[b][:nt, :], in0=ub4[b][:nt, :], in1=pb4[q][:nt, :], op=ALU.add),
                     reads=[f"ub{b}", f"pb{q}"], writes=[f"ub{b}"])
        layer_norm_tile(P, nc, (f"ub{b}", ub4[b]), nt, D, stats4, mv4, rstd4, gb4, bb4, LN_EPS)
        P.dma("sp", f"ub{b}:st", [(out[t0:t0 + nt, :], ub4[b][:nt, :])], reads=[f"ub{b}"], writes=["out:" + pfx])


DEC = 0.6065306597126334
GN_EPS = 64e-5


def setup_scan_consts(P, nc):
    def mk(name, pattern_fn):
        t = nc.alloc_sbuf_tensor(name, [128, 128], F32)
        P.op("pool", lambda e: e.memset(t[:], 1.0), writes=[name])
        pattern_fn(t)
        _consts[name] = t
        return t
    mk("triI", lambda t: P.op("pool", lambda e: e.affine_select(out=t[:], in_=t[:], pattern=[[1, 128]], compare_op=ALU.is_ge, fill=0.0,
                                                               base=0, channel_multiplier=-1), reads=["triI"], writes=["triI"]))
    mk("triE", lambda t: P.op("pool", lambda e: e.affine_select(out=t[:], in_=t[:], pattern=[[1, 128]], compare_op=ALU.is_gt, fill=0.0,
                                                               base=0, channel_multiplier=-1), reads=["triE"], writes=["triE"]))
    mk("triR", lambda t: P.op("pool", lambda e: e.affine_select(out=t[:], in_=t[:], pattern=[[-1, 128]], compare_op=ALU.is_gt, fill=0.0,
                                                               base=0, channel_multiplier=1), reads=["triR"], writes=["triR"]))
    ones = nc.alloc_sbuf_tensor("onesc", [128, 1], F32)
    P.op("pool", lambda e: e.memset(ones[:], 1.0), writes=["onesc"])
    _consts["onesc"] = ones
    m4 = nc.alloc_sbuf_tensor("mask4", [128, 512], F32)
    _consts["mask4"] = m4
    for q in range(4):
        src = "triE" if q % 2 == 0 else "triI"
        P.op("pool", lambda e, q=q, src=src: e.tensor_copy(out=m4[:, q * 128:(q + 1) * 128], in_=_consts[src][:]), reads=[src], writes=["mask4"])


import os
SSTOP = int(os.environ.get('SSTOP', '0'))


def emit_scan(P, nc, A, PS, NT, HC, HG, Rd, Kd, Vd, WLd, ALd, Gd, vecs, zT_out, ST):
    FG = HG * 64
    NP = HG // 2
    NG = HC // HG
    psn = [f"ps{i}" for i in range(8)]
    ident = _consts["ident"]
    triI, triE, triR, onesc, mask4 = (_consts[k] for k in ("triI", "triE", "triR", "onesc", "mask4"))
    fence(P)
    A.reset()
    F_ = lambda: A.take([128, FG], F32)
    vb = {k: [F_() for _ in range(NG)] for k in vecs}
    inR, inK, inV, inW, inA, inG = F_(), F_(), F_(), F_(), F_(), F_()
    sg, kk, bq, t1, cI, cE, cR = F_(), F_(), F_(), F_(), F_(), F_(), F_()
    Rt, At, Bt, Kt, Bp, Kp = F_(), F_(), F_(), F_(), F_(), F_()
    yb, zb = F_(), F_()
    sm = [A.take([128, HG], F32) for _ in range(6)]
    FT = A.take([128, NP * 4, 128], F32)
    Msb = A.take([128, HG, 512], F32)
    Xs = A.take([128, HG, 128], F32)
    Ns = A.take([128, HG, 128], F32)
    Ts = A.take([128, HG, 128], F32)
    U0 = A.take([128, HG, 64], F32)
    Us = A.take([128, HG, 64], F32)
    wc = A.take([128, NP], F32)
    zTs = A.take([128, NP, 128], F32)
    for k, d in vecs.items():
        for g in range(NG):
            P.dma("sp", f"vb_{k}{g}", [(vb[k][g][:, :], d[g * FG:(g + 1) * FG].partition_broadcast(128))], writes=[f"vb_{k}{g}"])
    P.op("dve", lambda e: e.memset(ST[:], 0.0), writes=["ST"])
    v3 = lambda ap: ap.rearrange("p (h n) -> p h n", n=64)
    bc = lambda ap: ap.unsqueeze(2).to_broadcast([128, HG, 64])
    pcnt = [0]

    def nextps():
        pcnt[0] += 1
        return pcnt[0] % 8

    def tt(eng, out, a, b, op, rd, wr):
        return P.op(eng, lambda e: e.tensor_tensor(out=out, in0=a, in1=b, op=op), reads=rd, writes=wr)

    for ti in range(NT):
        for g in range(NG):
            r0 = ti * 128
            c0 = g * FG
            vg = {k: (f"vb_{k}{g}", vb[k][g]) for k in vecs}
            for nm, buf, src in (("inR", inR, Rd), ("inK", inK, Kd), ("inV", inV, Vd), ("inW", inW, WLd), ("inA", inA, ALd), ("inG", inG, Gd)):
                P.dma("sp", nm, [(buf[:, :], src[r0:r0 + 128, c0:c0 + FG])], writes=[nm])
            tt("dve", sg[:, :], inW[:, :], vg["w0"][1][:, :], ALU.add, ["inW", vg["w0"][0]], ["sg"])
            P.op("act", lambda e: e.activation(out=sg[:, :], in_=sg[:, :], func=AF.Sigmoid), reads=["sg"], writes=["sg"])
            tt("dve", inA[:, :], inA[:, :], vg["a0"][1][:, :], ALU.add, ["inA", vg["a0"][0]], ["inA"])
            P.op("act", lambda e: e.activation(out=inA[:, :], in_=inA[:, :], func=AF.Sigmoid), reads=["inA"], writes=["inA"])
            tt("dve", kk[:, :], inK[:, :], vg["k_k"][1][:, :], ALU.mult, ["inK", vg["k_k"][0]], ["kk"])
            tt("pool", t1[:, :], kk[:, :], kk[:, :], ALU.mult, ["kk"], ["t1"])
            P.op("dve", lambda e: e.tensor_reduce(out=sm[0][:, :], in_=v3(t1[:, :]), axis=AX.X, op=ALU.add), reads=["t1"], writes=["sm0"])
            P.op("act", lambda e: e.activation(out=sm[0][:, :], in_=sm[0][:, :], func=AF.Sqrt), reads=["sm0"], writes=["sm0"])
            P.op("dve", lambda e: e.tensor_scalar(out=sm[0][:, :], in0=sm[0][:, :], scalar1=1e-12, scalar2=None, op0=ALU.max), reads=["sm0"], writes=["sm0"])
            P.op("dve", lambda e: e.reciprocal(out=sm[0][:, :], in_=sm[0][:, :]), reads=["sm0"], writes=["sm0"])
            tt("dve", v3(kk[:, :]), v3(kk[:, :]), bc(sm[0][:, :]), ALU.mult, ["kk", "sm0"], ["kk"])
            P.op("dve", lambda e, vg=vg: e.scalar_tensor_tensor(out=t1[:, :], in0=inA[:, :], scalar=-1.0, in1=vg["k_a"][1][:, :], op0=ALU.add, op1=ALU.mult),
                 reads=["inA", vg["k_a"][0]], writes=["t1"])
            P.op("dve", lambda e: e.scalar_tensor_tensor(out=inK[:, :], in0=t1[:, :], scalar=1.0, in1=inK[:, :], op0=ALU.add, op1=ALU.mult),
                 reads=["t1", "inK"], writes=["inK"])
            tt("pool", bq[:, :], kk[:, :], inA[:, :], ALU.mult, ["kk", "inA"], ["bq"])
            if SSTOP == 1: continue
            for (cbuf, cname, tri, trin, sc) in ((cI, "cI", triI, "triI", -DEC), (cE, "cE", triE, "triE", -DEC), (cR, "cR", triR, "triR", -DEC)):
                for q0 in range(0, FG, 512):
                    qn = min(512, FG - q0)
                    pi = nextps()
                    P.op("pe", lambda e, pi=pi, tri=tri, q0=q0, qn=qn: e.matmul(PS[pi][:, :qn], lhsT=tri[:, :], rhs=sg[:, q0:q0 + qn], start=True, stop=True),
                         reads=["sg", trin], writes=[psn[pi]])
                    P.op("act", lambda e, pi=pi, cbuf=cbuf, q0=q0, qn=qn, sc=sc: e.activation(out=cbuf[:, q0:q0 + qn], in_=PS[pi][:, :qn], func=AF.Exp, scale=sc),
                         reads=[psn[pi]], writes=[cname])
            if SSTOP == 2: continue
            P.op("dve", lambda e: e.reciprocal(out=t1[:, :], in_=cI[:, :]), reads=["cI"], writes=["t1"])
            tt("dve", Rt[:, :], inR[:, :], cI[:, :], ALU.mult, ["inR", "cI"], ["Rt"])
            P.op("dve", lambda e: e.scalar_tensor_tensor(out=At[:, :], in0=kk[:, :], scalar=-1.0, in1=cE[:, :], op0=ALU.mult, op1=ALU.mult),
                 reads=["kk", "cE"], writes=["At"])
            tt("pool", Bt[:, :], bq[:, :], t1[:, :], ALU.mult, ["bq", "t1"], ["Bt"])
            tt("dve", Kt[:, :], inK[:, :], t1[:, :], ALU.mult, ["inK", "t1"], ["Kt"])
            tt("pool", Bp[:, :], bq[:, :], cR[:, :], ALU.mult, ["bq", "cR"], ["Bp"])
            tt("dve", Kp[:, :], inK[:, :], cR[:, :], ALU.mult, ["inK", "cR"], ["Kp"])
            if SSTOP == 3: continue
            pi = nextps()
            P.op("pe", [(lambda e, pi=pi, p=p: e.matmul(PS[pi][:, p:p + 1], lhsT=sg[:, p * 128:(p + 1) * 128], rhs=onesc[:, :], start=True, stop=True))
                        for p in range(NP)], reads=["sg", "onesc"], writes=[psn[pi]])
            P.op("act", lambda e, pi=pi: e.activation(out=wc[:, :], in_=PS[pi][:, :NP], func=AF.Exp, scale=-DEC), reads=[psn[pi]], writes=["wc"])
            if SSTOP == 4: continue
            for p in range(NP):
                pi = nextps()
                srcs = ((Bt, "Bt"), (Kt, "Kt"), (At, "At"), (Rt, "Rt"))
                P.op("pe", [(lambda e, pi=pi, p=p, q=q, sb=sb: e.transpose(out=PS[pi][:, q * 128:(q + 1) * 128], in_=sb[:, p * 128:(p + 1) * 128], identity=ident[:, :]))
                            for q, (sb, _) in enumerate(srcs)], reads=["Bt", "Kt", "At", "Rt", "ident"], writes=[psn[pi]])
                eng = "act" if p % 2 else "dve"
                if eng == "act":
                    P.op("act", lambda e, pi=pi, p=p: e.copy(out=FT[:, p * 4:(p + 1) * 4, :], in_=PS[pi][:, :].rearrange("p (a b) -> p a b", a=4)), reads=[psn[pi]], writes=["FT"])
                else:
                    P.op("dve", lambda e, pi=pi, p=p: e.tensor_copy(out=FT[:, p * 4:(p + 1) * 4, :], in_=PS[pi][:, :].rearrange("p (a b) -> p a b", a=4)), reads=[psn[pi]], writes=["FT"])
            if SSTOP == 5: continue
            for h in range(HG):
                p, hh = h // 2, h % 2
                lo = hh * 64
                if os.environ.get("EVENONLY") == "1" and hh == 1: continue
                pi = nextps()
                fns = [lambda e, pi=pi, p=p, lo=lo: e.matmul(PS[pi][:, 0:256], lhsT=FT[lo:lo + 64, p * 4 + 0, :], rhs=FT[lo:lo + 64, p * 4 + 2:p * 4 + 4, :].rearrange("p a b -> p (a b)"), start=True, stop=True),
                       lambda e, pi=pi, p=p, lo=lo: e.matmul(PS[pi][:, 256:512], lhsT=FT[lo:lo + 64, p * 4 + 1, :], rhs=FT[lo:lo + 64, p * 4 + 2:p * 4 + 4, :].rearrange("p a b -> p (a b)"), start=True, stop=True)]
                P.op("pe", fns, reads=["FT"], writes=[psn[pi]])
                P.op("dve", lambda e, pi=pi, h=h: e.tensor_tensor(out=Msb[:, h, :], in0=PS[pi][:, :], in1=mask4[:, :], op=ALU.mult), reads=[psn[pi], "mask4"], writes=["Msb"])
            if SSTOP == 55: continue
            for h0 in range(0, HG, 8):
                hs8 = list(range(h0, min(h0 + 8, HG)))
                for par in (0, 1):
                    hp = [h for h in hs8 if h % 2 == par]
                    if not hp:
                        continue
                    pi = nextps()
                    lo = par * 64
                    P.op("pe", [(lambda e, pi=pi, p=h // 2, lo=lo, q=q: e.matmul(PS[pi][:, q * 128:(q + 1) * 128], lhsT=FT[lo:lo + 64, p * 4 + 2, :], rhs=FT[lo:lo + 64, p * 4 + 0, :], start=True, stop=True))
                                for q, h in enumerate(hp)], reads=["FT"], writes=[psn[pi]])
                    P.op("dve", [(lambda e, pi=pi, h=h, q=q: e.tensor_tensor(out=Ns[:, h, :], in0=PS[pi][:, q * 128:(q + 1) * 128], in1=triR[:, :], op=ALU.mult)) for q, h in enumerate(hp)],
                         reads=[psn[pi], "triR"], writes=["Ns"])
            if SSTOP == 6: continue
            P.op("pool", lambda e: e.tensor_copy(out=Xs[:, :, :], in_=Msb[:, :, 0:128]), reads=["Msb"], writes=["Xs"])
            P.op("dve", [(lambda e, h=h: e.tensor_tensor(out=Ts[:, h, :], in0=Msb[:, h, 0:128], in1=ident[:, :], op=ALU.add)) for h in range(HG)],
                 reads=["Msb", "ident"], writes=["Ts"])
            for k in range(1, 7):
                for h0 in range(0, HG, 4):
                    nh = min(4, HG - h0)
                    pa, pb, pc = nextps(), nextps(), nextps()
                    needX = k <= 5
                    P.op("pe", [(lambda e, pb=pb, h=h, q=h - h0: e.matmul(PS[pb][:, q * 128:(q + 1) * 128], lhsT=Xs[:, h, :], rhs=Ns[:, h, :], start=True, stop=True)) for h in range(h0, h0 + nh)],
                         reads=["Xs", "Ns"], writes=[psn[pb]])
                    if needX:
                        P.op("pe", [(lambda e, pa=pa, h=h, q=h - h0: e.matmul(PS[pa][:, q * 128:(q + 1) * 128], lhsT=Ns[:, h, :], rhs=Xs[:, h, :], start=True, stop=True)) for h in range(h0, h0 + nh)],
                             reads=["Xs", "Ns"], writes=[psn[pa]])
                    P.op("act", lambda e, pb=pb, h0=h0, nh=nh: e.copy(out=Ns[:, h0:h0 + nh, :], in_=PS[pb][:, :nh * 128].rearrange("p (a b) -> p a b", a=nh)), reads=[psn[pb]], writes=["Ns"])
                    if needX:
                        P.op("dve", lambda e, pa=pa, h0=h0, nh=nh: e.tensor_copy(out=Xs[:, h0:h0 + nh, :], in_=PS[pa][:, :nh * 128].rearrange("p (a b) -> p a b", a=nh)), reads=[psn[pa]], writes=["Xs"])
                    P.op("pe", [(lambda e, pc=pc, h=h, q=h - h0: e.matmul(PS[pc][:, q * 128:(q + 1) * 128], lhsT=Ns[:, h, :], rhs=Ts[:, h, :], start=True, stop=True)) for h in range(h0, h0 + nh)],
                         reads=["Ns", "Ts"], writes=[psn[pc]])
                    P.op("dve", lambda e, pc=pc, h0=h0, nh=nh: e.tensor_tensor(out=Ts[:, h0:h0 + nh, :], in0=Ts[:, h0:h0 + nh, :], in1=PS[pc][:, :nh * 128].rearrange("p (a b) -> p a b", a=nh), op=ALU.add),
                         reads=[psn[pc], "Ts"], writes=["Ts"])
            if SSTOP == 7: continue
            for h0 in range(0, HG, 8):
                nh = min(8, HG - h0)
                hs = list(range(h0, h0 + nh))
                sp = lambda h: ((g * HG + h) // 2, ((g * HG + h) % 2) * 64)
                pi = nextps()
                fns = []
                for h in hs:
                    p, lo = h // 2, (h % 2) * 64
                    spi, slo = sp(h)
                    q = h - h0
                    fns.append(lambda e, pi=pi, p=p, lo=lo, spi=spi, q=q: e.matmul(PS[pi][:, q * 64:(q + 1) * 64], lhsT=FT[lo:lo + 64, p * 4 + 2, :], rhs=ST[lo:lo + 64, spi, :], start=True, stop=False))
                    fns.append(lambda e, pi=pi, h=h, q=q: e.matmul(PS[pi][:, q * 64:(q + 1) * 64], lhsT=Msb[:, h, 256:384], rhs=inV[:, h * 64:(h + 1) * 64], start=False, stop=True))
                P.op("pe", fns, reads=["FT", "ST", "Msb", "inV"], writes=[psn[pi]])
                P.op("act", lambda e, pi=pi, h0=h0, nh=nh: e.copy(out=U0[:, h0:h0 + nh, :], in_=PS[pi][:, :nh * 64].rearrange("p (a b) -> p a b", a=nh)), reads=[psn[pi]], writes=["U0"])
                pi = nextps()
                P.op("pe", [(lambda e, pi=pi, h=h, q=h - h0: e.matmul(PS[pi][:, q * 64:(q + 1) * 64], lhsT=Ts[:, h, :], rhs=U0[:, h, :], start=True, stop=True)) for h in hs],
                     reads=["Ts", "U0"], writes=[psn[pi]])
                P.op("dve", lambda e, pi=pi, h0=h0, nh=nh: e.tensor_copy(out=Us[:, h0:h0 + nh, :], in_=PS[pi][:, :nh * 64].rearrange("p (a b) -> p a b", a=nh)), reads=[psn[pi]], writes=["Us"])
                pi = nextps()
                fns = []
                for h in hs:
                    p, lo = h // 2, (h % 2) * 64
                    spi, slo = sp(h)
                    q = h - h0
                    fns.append(lambda e, pi=pi, p=p, lo=lo, spi=spi, q=q: e.matmul(PS[pi][:, q * 64:(q + 1) * 64], lhsT=FT[lo:lo + 64, p * 4 + 3, :], rhs=ST[lo:lo + 64, spi, :], start=True, stop=False))
                    fns.append(lambda e, pi=pi, h=h, q=q: e.matmul(PS[pi][:, q * 64:(q + 1) * 64], lhsT=Msb[:, h, 128:256], rhs=Us[:, h, :], start=False, stop=False))
                    fns.append(lambda e, pi=pi, h=h, q=q: e.matmul(PS[pi][:, q * 64:(q + 1) * 64], lhsT=Msb[:, h, 384:512], rhs=inV[:, h * 64:(h + 1) * 64], start=False, stop=True))
                P.op("pe", fns, reads=["FT", "ST", "Msb", "Us", "inV"], writes=[psn[pi]])
                P.op("act", lambda e, pi=pi, h0=h0, nh=nh: e.copy(out=yb[:, h0 * 64:(h0 + nh) * 64], in_=PS[pi][:, :nh * 64]), reads=[psn[pi]], writes=["yb"])
                pi = nextps()
                fns = []
                for h in hs:
                    p = h // 2
                    q = h - h0
                    fns.append(lambda e, pi=pi, p=p, h=h, q=q: e.matmul(PS[pi][:, q * 64:(q + 1) * 64], lhsT=Bp[:, p * 128:(p + 1) * 128], rhs=Us[:, h, :], start=True, stop=False))
                    fns.append(lambda e, pi=pi, p=p, h=h, q=q: e.matmul(PS[pi][:, q * 64:(q + 1) * 64], lhsT=Kp[:, p * 128:(p + 1) * 128], rhs=inV[:, h * 64:(h + 1) * 64], start=False, stop=True))
                P.op("pe", fns, reads=["Bp", "Kp", "Us", "inV"], writes=[psn[pi]])
                for h in hs:
                    p, lo = h // 2, (h % 2) * 64
                    spi, slo = sp(h)
                    q = h - h0
                    P.op("dve", lambda e, pi=pi, p=p, lo=lo, spi=spi, q=q: e.scalar_tensor_tensor(out=ST[lo:lo + 64, spi, :], in0=ST[lo:lo + 64, spi, :], scalar=wc[lo:lo + 64, p:p + 1],
                                                                                              in1=PS[pi][lo:lo + 64, q * 64:(q + 1) * 64], op0=ALU.mult, op1=ALU.add),
                         reads=["ST", "wc", psn[pi]], writes=["ST"])
            if SSTOP == 8: continue
            P.op("dve", lambda e: e.tensor_reduce(out=sm[1][:, :], in_=v3(yb[:, :]), axis=AX.X, op=ALU.add), reads=["yb"], writes=["sm1"])
            tt("pool", t1[:, :], yb[:, :], yb[:, :], ALU.mult, ["yb"], ["t1"])
            P.op("dve", lambda e: e.tensor_reduce(out=sm[2][:, :], in_=v3(t1[:, :]), axis=AX.X, op=ALU.add), reads=["t1"], writes=["sm2"])
            P.op("dve", lambda e: e.tensor_scalar(out=sm[1][:, :], in0=sm[1][:, :], scalar1=1.0 / 64, scalar2=None, op0=ALU.mult), reads=["sm1"], writes=["sm1"])
            tt("dve", sm[3][:, :], sm[1][:, :], sm[1][:, :], ALU.mult, ["sm1"], ["sm3"])
            P.op("dve", lambda e: e.scalar_tensor_tensor(out=sm[2][:, :], in0=sm[2][:, :], scalar=1.0 / 64, in1=sm[3][:, :], op0=ALU.mult, op1=ALU.subtract),
                 reads=["sm2", "sm3"], writes=["sm2"])
            P.op("dve", lambda e: e.tensor_scalar(out=sm[2][:, :], in0=sm[2][:, :], scalar1=GN_EPS, scalar2=None, op0=ALU.add), reads=["sm2"], writes=["sm2"])
            P.op("act", lambda e: e.activation(out=sm[2][:, :], in_=sm[2][:, :], func=AF.Sqrt), reads=["sm2"], writes=["sm2"])
            P.op("dve", lambda e: e.reciprocal(out=sm[2][:, :], in_=sm[2][:, :]), reads=["sm2"], writes=["sm2"])
            tt("dve", v3(yb[:, :]), v3(yb[:, :]), bc(sm[1][:, :]), ALU.subtract, ["yb", "sm1"], ["yb"])
            tt("dve", v3(yb[:, :]), v3(yb[:, :]), bc(sm[2][:, :]), ALU.mult, ["yb", "sm2"], ["yb"])
            tt("pool", yb[:, :], yb[:, :], vg["gn_g"][1][:, :], ALU.mult, ["yb", vg["gn_g"][0]], ["yb"])
            tt("dve", yb[:, :], yb[:, :], vg["gn_b"][1][:, :], ALU.add, ["yb", vg["gn_b"][0]], ["yb"])
            tt("pool", t1[:, :], inR[:, :], inK[:, :], ALU.mult, ["inR", "inK"], ["t1"])
            tt("dve", t1[:, :], t1[:, :], vg["r_k"][1][:, :], ALU.mult, ["t1", vg["r_k"][0]], ["t1"])
            P.op("dve", lambda e: e.tensor_reduce(out=sm[4][:, :], in_=v3(t1[:, :]), axis=AX.X, op=ALU.add), reads=["t1"], writes=["sm4"])
            tt("dve", v3(t1[:, :]), v3(inV[:, :]), bc(sm[4][:, :]), ALU.mult, ["inV", "sm4"], ["t1"])
            tt("dve", yb[:, :], yb[:, :], t1[:, :], ALU.add, ["yb", "t1"], ["yb"])
            tt("dve", zb[:, :], yb[:, :], inG[:, :], ALU.mult, ["yb", "inG"], ["zb"])
            if SSTOP == 9: continue
            for p0 in range(0, NP, 4):
                npp = min(4, NP - p0)
                pi = nextps()
                P.op("pe", [(lambda e, pi=pi, p=p, q=p - p0: e.transpose(out=PS[pi][:, q * 128:(q + 1) * 128], in_=zb[:, p * 128:(p + 1) * 128], identity=ident[:, :])) for p in range(p0, p0 + npp)],
                     reads=["zb", "ident"], writes=[psn[pi]])
                P.op("act", lambda e, pi=pi, p0=p0, npp=npp: e.copy(out=zTs[:, p0:p0 + npp, :], in_=PS[pi][:, :npp * 128].rearrange("p (a b) -> p a b", a=npp)), reads=[psn[pi]], writes=["zTs"])
            P.dma("sp", "zTs:st", [(zT_out[c0:c0 + FG, r0:r0 + 128].rearrange("(a p) t -> p a t", p=128), zTs[:, :, :])], reads=["zTs"])


def emit_rwkv_proj(P, nc, A, PS, D, FC, halves, hTz, muT, w_rkv_c, w1, w2_c, a1, a2_c, g1, g2_c, outs):
    KT = D // 128
    LW, LA, LG = w1.shape[1], a1.shape[1], g1.shape[1]
    assert LW <= 128 and LA <= 128
    GT = (LG + 127) // 128
    psn = [f"ps{i}" for i in range(8)]
    Rd, Kd, Vd, WLd, ALd, Gd = outs
    NC = (FC + 511) // 512
    fence(P)
    A.reset()
    nth_max = max(sum(nt for _, nt in tl) for tl in halves)
    XT = A.take([128, KT, nth_max], BF16)
    slabs = [A.take([128, KT, 512], BF16) for _ in range(2)]
    cur = [A.take([128, nth_max], F32) for _ in range(2)]
    prv = [A.take([128, nth_max], F32) for _ in range(2)]
    stage = [A.take([128, 512], F32) for _ in range(2)]
    mus = A.take([128, 6 * KT], F32)
    lT = A.take([128, GT, nth_max], BF16)
    w2s = A.take([128, GT, FC], BF16)
    P.dma("sp", "mus", [(mus[:, :], muT)], writes=["mus"])
    it = [0]
    si = [0]

    def evac_store(pi, nt, dst):
        b = it[0] % 2
        it[0] += 1
        if b:
            P.op("act", lambda e: e.copy(out=stage[b][:nt, :dst.shape[1]], in_=PS[pi][:nt, :dst.shape[1]]), reads=[psn[pi]], writes=[f"stage{b}"])
        else:
            P.op("dve", lambda e: e.tensor_copy(out=stage[b][:nt, :dst.shape[1]], in_=PS[pi][:nt, :dst.shape[1]]), reads=[psn[pi]], writes=[f"stage{b}"])
        P.dma("sp", f"stage{b}:st", [(dst, stage[b][:nt, :dst.shape[1]])], reads=[f"stage{b}"])

    pcnt = [0]

    def nextps():
        pcnt[0] += 1
        return pcnt[0] % 8

    for tiles in halves:
        T0 = tiles[0][0]
        nth = sum(nt for _, nt in tiles)
        chunks = tok_chunks(tiles)
        for n in range(6):
            for kt in range(KT):
                b = kt % 2
                P.dma("sp", f"cur{b}", [(cur[b][:, :nth], hTz[kt * 128:(kt + 1) * 128, 1 + T0:1 + T0 + nth])], writes=[f"cur{b}"])
                P.dma("sp", f"prv{b}", [(prv[b][:, :nth], hTz[kt * 128:(kt + 1) * 128, T0:T0 + nth])], writes=[f"prv{b}"])
                P.op("pool", lambda e, b=b, nth=nth: e.tensor_tensor(out=prv[b][:, :nth], in0=prv[b][:, :nth], in1=cur[b][:, :nth], op=ALU.subtract),
                     reads=[f"prv{b}", f"cur{b}"], writes=[f"prv{b}"])
                P.op("dve", lambda e, b=b, kt=kt, n=n, nth=nth: e.scalar_tensor_tensor(out=XT[:, kt, :nth], in0=prv[b][:, :nth], scalar=mus[:, n * KT + kt:n * KT + kt + 1],
                                                                             in1=cur[b][:, :nth], op0=ALU.mult, op1=ALU.add),
                     reads=[f"prv{b}", f"cur{b}", "mus"], writes=["XT"])
            if n < 3:
                dst = (Rd, Kd, Vd)[n]
                for c in range(NC):
                    cw = min(512, FC - c * 512)
                    s = si[0] % 2
                    si[0] += 1
                    wv = w_rkv_c[n, :, c * 512:c * 512 + cw].rearrange("(kt p) n -> p kt n", p=128)
                    stp = max(1, KT // 4)
                    P.dma("pool", f"slab{s}", [(slabs[s][:, q:q + stp, :cw], wv[:, q:q + stp, :]) for q in range(0, KT, stp)], writes=[f"slab{s}"])
                    for (t0, nt) in tiles:
                        pi = nextps()
                        P.op("pe", [(lambda e, kt=kt, t0=t0 - T0, nt=nt, s=s, pi=pi, cw=cw: e.matmul(PS[pi][:nt, :cw], lhsT=XT[:, kt, t0:t0 + nt], rhs=slabs[s][:, kt, :cw],
                                                                                          start=(kt == 0), stop=(kt == KT - 1))) for kt in range(KT)],
                             reads=["XT", f"slab{s}"], writes=[psn[pi]])
                        evac_store(pi, nt, dst[t0:t0 + nt, c * 512:c * 512 + cw])
            else:
                wA, wB, dst, L, func = ((w1, w2_c, WLd, LW, AF.Tanh), (a1, a2_c, ALd, LA, AF.Copy), (g1, g2_c, Gd, LG, AF.Sigmoid))[n - 3]
                ltn = (L + 127) // 128
                s = si[0] % 2
                si[0] += 1
                wv = wA.rearrange("(kt p) n -> p kt n", p=128)
                stp = max(1, KT // 4)
                P.dma("pool", f"slab{s}", [(slabs[s][:, q:q + stp, :L], wv[:, q:q + stp, :]) for q in range(0, KT, stp)], writes=[f"slab{s}"])
                for lt in range(ltn):
                    lw = min(128, L - lt * 128)
                    P.dma("pool", "w2s", [(w2s[:lw, lt, :], wB[lt * 128:lt * 128 + lw, :])], writes=["w2s"])
                for lt in range(ltn):
                    lw = min(128, L - lt * 128)
                    for (c0, cn) in chunks:
                        pi = nextps()
                        P.op("pe", [(lambda e, kt=kt, s=s, lt=lt, lw=lw, c0=c0 - T0, cn=cn, pi=pi: e.matmul(PS[pi][:lw, :cn], lhsT=slabs[s][:, kt, lt * 128:lt * 128 + lw],
                                                                                                      rhs=XT[:, kt, c0:c0 + cn], start=(kt == 0), stop=(kt == KT - 1)))
                                    for kt in range(KT)], reads=["XT", f"slab{s}"], writes=[psn[pi]])
                        P.op("act", lambda e, lt=lt, lw=lw, c0=c0 - T0, cn=cn, pi=pi, func=func: e.activation(out=lT[:lw, lt, c0:c0 + cn], in_=PS[pi][:lw, :cn], func=func),
                             reads=[psn[pi]], writes=["lT"])
                for (t0, nt) in tiles:
                    for c in range(NC):
                        cw = min(512, FC - c * 512)
                        pi = nextps()
                        fns = []
                        for lt in range(ltn):
                            lw = min(128, L - lt * 128)
                            fns.append(lambda e, lt=lt, lw=lw, t0=t0 - T0, nt=nt, c=c, cw=cw, pi=pi: e.matmul(PS[pi][:nt, :cw], lhsT=lT[:lw, lt, t0:t0 + nt],
                                                                                                    rhs=w2s[:lw, lt, c * 512:c * 512 + cw], start=(lt == 0), stop=(lt == ltn - 1)))
                        P.op("pe", fns, reads=["lT", "w2s"], writes=[psn[pi]])
                        evac_store(pi, nt, dst[t0:t0 + nt, c * 512:c * 512 + cw])


def setup_attn_consts(P, nc):
    m = nc.alloc_sbuf_tensor("amask", [128, 4, 512], F32)
    P.op("pool", lambda e: e.memset(m[:], 1.0), writes=["amask"])
    for i in range(4):
        P.op("pool", lambda e, i=i: e.affine_select(out=m[:, i, :], in_=m[:, i, :], pattern=[[1, 512]], compare_op=ALU.is_gt, fill=0.0,
                                                   base=-128 * i, channel_multiplier=-1), reads=["amask"], writes=["amask"])
    _consts["amask"] = m
    o = nc.alloc_sbuf_tensor("ones128", [128, 128], F32)
    P.op("pool", lambda e: e.memset(o[:], 1.0), writes=["ones128"])
    _consts["ones128"] = o


def emit_attn(P, nc, A, PS, D, NH, NM, TQ, halves, hX, wq, wk, wv, oT_out, QTd, KTd, Vd):
    KT = D // 128
    TK = NM + TQ
    FC = NH * 128
    psn = [f"ps{i}" for i in range(8)]
    ident = _consts["ident"]
    triR, amask, ones128 = _consts["triR"], _consts["amask"], _consts["ones128"]
    scale = 128 ** -0.5
    fence(P)
    A.reset()
    nth_max = max(sum(nt for _, nt in tl) for tl in halves)
    XT = A.take([128, KT, nth_max], BF16)
    slabs = [A.take([128, KT, 512], BF16) for _ in range(2)]
    xin = [A.take([128, D], F32) for _ in range(2)]
    stg = [A.take([128, 512], BF16) for _ in range(2)]
    pcnt = [0]

    def nextps():
        pcnt[0] += 1
        return pcnt[0] % 8
    it = [0]
    si = [0]

    def evac_store_bf(pi, npart, ncol, dst):
        b = it[0] % 2
        it[0] += 1
        if b:
            P.op("act", lambda e: e.copy(out=stg[b][:npart, :ncol], in_=PS[pi][:npart, :ncol]), reads=[psn[pi]], writes=[f"stg{b}"])
        else:
            P.op("dve", lambda e: e.tensor_copy(out=stg[b][:npart, :ncol], in_=PS[pi][:npart, :ncol]), reads=[psn[pi]], writes=[f"stg{b}"])
        P.dma("sp", f"stg{b}:st", [(dst, stg[b][:npart, :ncol])], reads=[f"stg{b}"])

    for tiles in halves:
        T0 = tiles[0][0]
        chunks = tok_chunks(tiles)
        for i, (t0, nt) in enumerate(tiles):
            b = i % 2
            P.dma("sp", f"xin{b}", [(xin[b][:nt, :], hX[t0:t0 + nt, :])], writes=[f"xin{b}"])
            for k0 in range(0, KT, 4):
                kn = min(4, KT - k0)
                pi = nextps()
                P.op("pe", [(lambda e, j=j, b=b, nt=nt, k0=k0, pi=pi: e.transpose(out=PS[pi][:, j * 128:j * 128 + nt], in_=xin[b][:nt, (k0 + j) * 128:(k0 + j + 1) * 128],
                                                                               identity=ident[:nt, :nt])) for j in range(kn)], reads=[f"xin{b}", "ident"], writes=[psn[pi]])
                o0 = t0 - T0
                if (k0 // 4) % 2:
                    P.op("act", lambda e, pi=pi, k0=k0, kn=kn, o0=o0, nt=nt: e.copy(out=XT[:, k0:k0 + kn, o0:o0 + nt], in_=PS[pi][:, :kn * 128].rearrange("p (a b) -> p a b", a=kn)[:, :, :nt]),
                         reads=[psn[pi]], writes=["XT"])
                else:
                    P.op("dve", lambda e, pi=pi, k0=k0, kn=kn, o0=o0, nt=nt: e.tensor_copy(out=XT[:, k0:k0 + kn, o0:o0 + nt], in_=PS[pi][:, :kn * 128].rearrange("p (a b) -> p a b", a=kn)[:, :, :nt]),
                         reads=[psn[pi]], writes=["XT"])
        for which, w in (("q", wq), ("k", wk), ("v", wv)):
            for c in range((FC + 511) // 512):
                cw = min(512, FC - c * 512)
                s = si[0] % 2
                si[0] += 1
                wvw = w[:, c * 512:c * 512 + cw].rearrange("(kt p) n -> p kt n", p=128)
                stp = max(1, KT // 4)
                P.dma("pool", f"slab{s}", [(slabs[s][:, q:q + stp, :cw], wvw[:, q:q + stp, :]) for q in range(0, KT, stp)], writes=[f"slab{s}"])
                if which == "v":
                    for (t0, nt) in tiles:
                        pi = nextps()
                        P.op("pe", [(lambda e, kt=kt, o0=t0 - T0, nt=nt, s=s, pi=pi, cw=cw: e.matmul(PS[pi][:nt, :cw], lhsT=XT[:, kt, o0:o0 + nt], rhs=slabs[s][:, kt, :cw],
                                                                                                  start=(kt == 0), stop=(kt == KT - 1))) for kt in range(KT)],
                             reads=["XT", f"slab{s}"], writes=[psn[pi]])
                        evac_store_bf(pi, nt, cw, Vd[t0:t0 + nt, c * 512:c * 512 + cw])
                else:
                    for h4 in range(cw // 128):
                        hd = c * 4 + h4
                        for (c0, cn) in chunks:
                            if which == "q":
                                if c0 + cn <= NM:
                                    continue
                                assert c0 >= NM
                            pi = nextps()
                            P.op("pe", [(lambda e, kt=kt, s=s, h4=h4, o0=c0 - T0, cn=cn, pi=pi: e.matmul(PS[pi][:, :cn], lhsT=slabs[s][:, kt, h4 * 128:(h4 + 1) * 128],
                                                                                                       rhs=XT[:, kt, o0:o0 + cn], start=(kt == 0), stop=(kt == KT - 1)))
                                        for kt in range(KT)], reads=["XT", f"slab{s}"], writes=[psn[pi]])
                            if which == "q":
                                evac_store_bf(pi, 128, cn, QTd[hd, :, c0 - NM:c0 - NM + cn])
                            else:
                                evac_store_bf(pi, 128, cn, KTd[hd, :, c0:c0 + cn])

    emit_attn_core(P, nc, A, PS, NH, NM, TQ, QTd, KTd, Vd, oT_out)


def emit_attn_core(P, nc, A, PS, NH, NM, TQ, QTd, KTd, Vd, oT_out):
    TK = NM + TQ
    psn = [f"ps{i}" for i in range(8)]
    triR, amask, ones128 = _consts["triR"], _consts["amask"], _consts["ones128"]
    scale = 128 ** -0.5
    fence(P)
    A.reset()
    NTK = TQ // 128
    NCH = TQ // 512
    Qs = [A.take([128, TQ], BF16) for _ in range(2)]
    Ks = [A.take([128, TK], BF16) for _ in range(2)]
    Vs = [A.take([128, NTK + 1, 128], BF16) for _ in range(2)]
    Eb = [A.take([128, 512], F32) for _ in range(2)]
    Lb = [A.take([128, 512], F32) for _ in range(2)]
    Tb = [A.take([128, 512], F32) for _ in range(2)]
    Ab = [A.take([128, 512], BF16) for _ in range(2)]
    Racc = [A.take([128, 512], F32) for _ in range(2)]
    ost = [A.take([128, 512], F32) for _ in range(2)]
    one_b = A.take([128, 1], F32)
    P.op("dve", lambda e: e.memset(one_b[:, :], 1.0), writes=["one_b"])
    tcount = [0]
    ccount = [0]
    for h in range(NH):
        hb = h % 2
        P.dma("sp", f"Qs{hb}", [(Qs[hb][:, :], QTd[h, :, :])], writes=[f"Qs{hb}"])
        P.dma("sp", f"Ks{hb}", [(Ks[hb][:, :], KTd[h, :, :])], writes=[f"Ks{hb}"])
        pairs = [(Vs[hb][:, 1:, :], Vd[NM:, h * 128:(h + 1) * 128].rearrange("(a p) d -> p a d", p=128))]
        if NM:
            pairs.append((Vs[hb][:NM, 0, :], Vd[0:NM, h * 128:(h + 1) * 128]))
        P.dma("sp", f"Vs{hb}", pairs, writes=[f"Vs{hb}"])
        for c in range(NCH):
            cb = ccount[0] % 2
            ccount[0] += 1
            pso = 6 + cb
            order = [("d", 4 * c + i, i) for i in (3, 2, 1, 0)] + [("f", j, None) for j in range(4 * c - 1, -1, -1)]
            if NM:
                order.append(("m", None, None))
            for oi, (kind, j, mi) in enumerate(order):
                b = tcount[0] % 2
                tcount[0] += 1
                psz = 2 * b
                pss = 2 * b + 1
                first, last = oi == 0, oi == len(order) - 1
                if kind == "m":
                    ns, kc0, vslot = NM, 0, 0
                else:
                    ns, kc0, vslot = 128, NM + j * 128, 1 + j
                q0 = c * 512
                P.op("pe", lambda e, psz=psz, hb=hb, ns=ns, kc0=kc0, q0=q0: e.matmul(PS[psz][:ns, :], lhsT=Ks[hb][:, kc0:kc0 + ns], rhs=Qs[hb][:, q0:q0 + 512], start=True, stop=True),
                     reads=[f"Ks{hb}", f"Qs{hb}"], writes=[psn[psz]])
                P.op("act", lambda e, b=b, psz=psz, ns=ns: e.activation(out=Eb[b][:ns, :], in_=PS[psz][:ns, :], func=AF.Exp, scale=scale), reads=[psn[psz]], writes=[f"Eb{b}"])
                P.op("act", lambda e, b=b, ns=ns: e.activation(out=Lb[b][:ns, :], in_=Eb[b][:ns, :], func=AF.Ln, bias=one_b[:ns, :], scale=1.0), reads=[f"Eb{b}", "one_b"], writes=[f"Lb{b}"])
                if kind == "d":
                    P.op("pool", lambda e, b=b, mi=mi: e.tensor_tensor(out=Lb[b][:, :], in0=Lb[b][:, :], in1=amask[:, mi, :], op=ALU.mult), reads=[f"Lb{b}", "amask"], writes=[f"Lb{b}"])
                fns = [lambda e, pss=pss, b=b, ns=ns, first=first: e.matmul(PS[pss][:ns, :], lhsT=triR[:ns, :ns], rhs=Lb[b][:ns, :], start=True, stop=first)]
                rd = [f"Lb{b}", "triR"]
                if not first:
                    fns.append(lambda e, pss=pss, cb=cb, ns=ns: e.matmul(PS[pss][:ns, :], lhsT=ones128[:, :ns], rhs=Racc[cb][:, :], start=False, stop=True))
                    rd += [f"Racc{cb}", "ones128"]
                P.op("pe", fns, reads=rd, writes=[psn[pss]])
                P.op("dve", lambda e, b=b, psz=psz, ns=ns: e.scalar_tensor_tensor(out=Tb[b][:ns, :], in0=PS[psz][:ns, :], scalar=scale, in1=Lb[b][:ns, :], op0=ALU.mult, op1=ALU.subtract),
                     reads=[psn[psz], f"Lb{b}"], writes=[f"Tb{b}"])
                P.op("dve", lambda e, b=b, pss=pss, ns=ns: e.tensor_tensor(out=Tb[b][:ns, :], in0=Tb[b][:ns, :], in1=PS[pss][:ns, :], op=ALU.subtract),
                     reads=[psn[pss], f"Tb{b}"], writes=[f"Tb{b}"])
                P.op("act", lambda e, b=b, ns=ns: e.activation(out=Ab[b][:ns, :], in_=Tb[b][:ns, :], func=AF.Exp), reads=[f"Tb{b}"], writes=[f"Ab{b}"])
                if kind == "d":
                    P.op("pool", lambda e, b=b, mi=mi: e.tensor_tensor(out=Ab[b][:, :], in0=Ab[b][:, :], in1=amask[:, mi, :], op=ALU.mult), reads=[f"Ab{b}", "amask"], writes=[f"Ab{b}"])
                P.op("pe", lambda e, pso=pso, hb=hb, b=b, ns=ns, vslot=vslot, first=first, last=last: e.matmul(PS[pso][:, :], lhsT=Vs[hb][:ns, vslot, :], rhs=Ab[b][:ns, :], start=first, stop=last),
                     reads=[f"Vs{hb}", f"Ab{b}"], writes=[psn[pso]])
                if not last:
                    if first:
                        P.op("pool", lambda e, b=b, cb=cb: e.tensor_copy(out=Racc[cb][:, :], in_=Lb[b][:, :]), reads=[f"Lb{b}"], writes=[f"Racc{cb}"])
                    else:
                        P.op("pool", lambda e, b=b, cb=cb: e.tensor_tensor(out=Racc[cb][:, :], in0=Racc[cb][:, :], in1=Lb[b][:, :], op=ALU.add), reads=[f"Lb{b}", f"Racc{cb}"], writes=[f"Racc{cb}"])
            P.op("act", lambda e, cb=cb, pso=pso: e.copy(out=ost[cb][:, :], in_=PS[pso][:, :]), reads=[psn[pso]], writes=[f"ost{cb}"])
            P.dma("sp", f"ost{cb}:st", [(oT_out[h * 128:(h + 1) * 128, c * 512:(c + 1) * 512], ost[cb][:, :])], reads=[f"ost{cb}"])


D_MODEL = 4096
SEQ = 2048
N_META = 16
D_FF = 16384
TPAD = 2176
NTOK = 1040
_VN = ("w0", "a0", "k_k", "k_a", "r_k", "gn_g", "gn_b")
_progs = {}


def _new_prog():
    nc = bass.Bass("TRN2", target_bir_lowering=False)
    P = Prog(nc)
    PS = [nc.alloc_psum_tensor(f"ps{i}", [128, 512], F32) for i in range(8)]
    setup_consts(P, nc)
    return nc, P, PS


def _arena(nc):
    words = (nc.sbuf_bytes_remaining - 2048) // 4
    return Arena(nc, words)


def build_rwkv():
    nc, P, PS = _new_prog()
    setup_scan_consts(P, nc)
    D, FC = D_MODEL, D_MODEL // 2
    inp = lambda name, shape: nc.dram_tensor(name, shape, F32, kind="ExternalInput").ap()
    hTz = inp("hTz", [D, 1 + TPAD])
    muT = inp("muT", [128, 6 * (D // 128)])
    w_rkv = inp("w_rkv", [3, D, FC])
    w1 = inp("w1", [D, 128]); w2 = inp("w2", [128, FC])
    a1 = inp("a1", [D, 128]); a2 = inp("a2", [128, FC])
    g1 = inp("g1", [D, 480]); g2 = inp("g2", [480, FC])
    vecs = {k: inp("v_" + k, [FC]) for k in _VN}
    zT = nc.dram_tensor("zT", [FC, TPAD], F32, kind="ExternalOutput").ap()
    scr = tuple(nc.dram_tensor("scr_" + n, [TPAD, FC], F32).ap() for n in ("R", "K", "V", "WL", "AL", "G"))
    ST = nc.alloc_sbuf_tensor("ST", [128, 16, 64], F32)
    A = _arena(nc)
    tiles = [(i * 128, 128) for i in range(TPAD // 128)]
    emit_rwkv_proj(P, nc, A, PS, D, FC, [tiles[:9], tiles[9:]], hTz, muT, w_rkv, w1, w2, a1, a2, g1, g2, scr)
    emit_scan(P, nc, A, PS, TPAD // 128, 32, 8, *scr, vecs, zT, ST)
    P.build()
    return nc


def build_dense():
    nc, P, PS = _new_prog()
    D, F, NE = D_MODEL, D_FF, 8
    inp = lambda name, shape: nc.dram_tensor(name, shape, F32, kind="ExternalInput").ap()
    zT = inp("zT", [D, NTOK]); h = inp("h", [NTOK, D]); w_o = inp("w_o", [D, D])
    g1 = inp("g1", [D]); b1 = inp("b1", [D]); g2 = inp("g2", [D]); b2 = inp("b2", [D])
    w_up = inp("w_up", [D, F]); w_down = inp("w_down", [F, D])
    out = nc.dram_tensor("out", [NTOK, D], F32, kind="ExternalOutput").ap()
    u1 = nc.dram_tensor("u1", [NTOK, D], F32).ap()
    h1s = nc.dram_tensor("h1s", [NTOK, D], F32).ap()
    parts = [nc.dram_tensor(f"part{e}", [NTOK, D], F32).ap() for e in range(NE)]
    A = _arena(nc)
    tiles = [(0, 16)] + [(16 + 128 * i, 128) for i in range(8)]
    emit_dense(P, nc, A, PS, "o", D, F, NE, tiles, zT, h, w_o, g1, b1, w_up, w_down, g2, b2, out, u1, h1s, parts)
    P.build()
    return nc


def build_attn():
    nc, P, PS = _new_prog()
    setup_scan_consts(P, nc)
    setup_attn_consts(P, nc)
    D, NH, NM, TQ = D_MODEL, 16, N_META, SEQ
    FC = NH * 128
    inp = lambda name, shape: nc.dram_tensor(name, shape, F32, kind="ExternalInput").ap()
    hX = inp("hX", [NM + TQ, D]); wq = inp("wq", [D, FC]); wk = inp("wk", [D, FC]); wv = inp("wv", [D, FC])
    oT = nc.dram_tensor("oT", [FC, TQ], F32, kind="ExternalOutput").ap()
    QTd = nc.dram_tensor("QTd", [NH, 128, TQ], BF16).ap()
    KTd = nc.dram_tensor("KTd", [NH, 128, NM + TQ], BF16).ap()
    Vd = nc.dram_tensor("Vd", [NM + TQ, FC], BF16).ap()
    A = _arena(nc)
    halves = [[(0, 16)] + [(16 + 128 * i, 128) for i in range(8)], [(16 + 128 * i, 128) for i in range(8, 16)]]
    emit_attn(P, nc, A, PS, D, NH, NM, TQ, halves, hX, wq, wk, wv, oT, QTd, KTd, Vd)
    P.build()
    return nc


def _get(name, fn):
    if name not in _progs:
        _progs[name] = fn()
    return _progs[name]


def _run(nc, in_maps):
    res = run_bass_kernel_spmd(nc, in_maps, core_ids=list(range(8)))
    return res.results


def kernel(x, meta_tokens, ln_mix_g, ln_mix_b, ln_ffn_g, ln_ffn_b, w_up, w_down,
           rwkv_mu, rwkv_w_rkv, rwkv_w0, rwkv_w1, rwkv_w2, rwkv_a0, rwkv_a1, rwkv_a2,
           rwkv_g1, rwkv_g2, rwkv_k_k, rwkv_k_a, rwkv_r_k, rwkv_gn_g, rwkv_gn_b, rwkv_w_o,
           sb_w_qkv, sb_w_o):
    f32 = np.float32
    c_ = np.ascontiguousarray
    x = np.asarray(x, f32)
    meta = np.asarray(meta_tokens, f32)
    B, D, H2 = x.shape[0], D_MODEL, D_MODEL // 2
    cores = [(c // 2, c % 2) for c in range(8)]
    KT = D // 128

    muT = c_(np.asarray(rwkv_mu[0], f32).reshape(6, KT, 128).transpose(2, 0, 1).reshape(128, 6 * KT))
    vec_src = dict(w0=rwkv_w0[0], a0=rwkv_a0[0], k_k=rwkv_k_k[0], k_a=rwkv_k_a[0], r_k=np.asarray(rwkv_r_k[0]).reshape(-1),
                   gn_g=rwkv_gn_g[0], gn_b=rwkv_gn_b[0])
    hTz = []
    for b in range(B):
        t = np.zeros((D, 1 + TPAD), f32)
        t[:, 1:1 + N_META] = meta.T
        t[:, 1 + N_META:1 + N_META + SEQ] = x[b].T
        hTz.append(t)
    in_maps = []
    for (b, p) in cores:
        sl = slice(p * H2, (p + 1) * H2)
        m = dict(hTz=hTz[b], muT=muT, w_rkv=c_(np.asarray(rwkv_w_rkv[0], f32)[:, :, sl]),
                 w1=c_(np.asarray(rwkv_w1[0], f32)), w2=c_(np.asarray(rwkv_w2[0], f32)[:, sl]),
                 a1=c_(np.asarray(rwkv_a1[0], f32)), a2=c_(np.asarray(rwkv_a2[0], f32)[:, sl]),
                 g1=c_(np.asarray(rwkv_g1[0], f32)), g2=c_(np.asarray(rwkv_g2[0], f32)[:, sl]))
        for k in _VN:
            m["v_" + k] = c_(np.asarray(vec_src[k], f32)[sl])
        in_maps.append(m)
    r1 = _run(_get("rwkv", build_rwkv), in_maps)
    del in_maps, hTz
    zfull = [np.concatenate([r1[2 * b]["zT"], r1[2 * b + 1]["zT"]], axis=0) for b in range(B)]

    def dense_maps(zT_list, h_list, w_o, li):
        maps = []
        for ci in range(8):
            maps.append(dict(zT=zT_list[ci], h=h_list[ci], w_o=w_o, g1=c_(np.asarray(ln_mix_g[li], f32)), b1=c_(np.asarray(ln_mix_b[li], f32)),
                             g2=c_(np.asarray(ln_ffn_g[li], f32)), b2=c_(np.asarray(ln_ffn_b[li], f32)),
                             w_up=c_(np.asarray(w_up[li], f32)), w_down=c_(np.asarray(w_down[li], f32))))
        return maps
    zT_l, h_l = [], []
    for (b, p) in cores:
        lo = N_META + p * 1024
        zT_l.append(c_(np.concatenate([zfull[b][:, :N_META], zfull[b][:, lo:lo + 1024]], axis=1)))
        h_l.append(c_(np.concatenate([meta, x[b, p * 1024:(p + 1) * 1024]], axis=0)))
    r2 = _run(_get("dense", build_dense), dense_maps(zT_l, h_l, c_(np.asarray(rwkv_w_o[0], f32)), 0))
    h2 = [r["out"] for r in r2]
    del zT_l, h_l, zfull

    wqkv = np.asarray(sb_w_qkv[0], f32)
    in_maps = []
    for (b, p) in cores:
        hX = c_(np.concatenate([h2[2 * b], h2[2 * b + 1][N_META:]], axis=0))
        sl = lambda w: c_(wqkv[:, w * D + p * H2: w * D + (p + 1) * H2])
        in_maps.append(dict(hX=hX, wq=sl(0), wk=sl(1), wv=sl(2)))
    r3 = _run(_get("attn", build_attn), in_maps)
    del in_maps
    ofull = [np.concatenate([r3[2 * b]["oT"], r3[2 * b + 1]["oT"]], axis=0) for b in range(B)]

    zT_l = []
    for (b, p) in cores:
        zT_l.append(c_(np.concatenate([np.zeros((D, N_META), f32), ofull[b][:, p * 1024:(p + 1) * 1024]], axis=1)))
    r4 = _run(_get("dense", build_dense), dense_maps(zT_l, h2, c_(np.asarray(sb_w_o[0], f32)), 1))
    out = np.empty((B, SEQ, D), f32)
    for ci, (b, p) in enumerate(cores):
        out[b, p * 1024:(p + 1) * 1024] = r4[ci]["out"][N_META:]
    return out
```
